# Optimizing a Trainium2 kernel written in Bass

```python
import numpy as np
import jax
import jax.numpy as jnp
from jax import lax

D_MODEL = 1024
BATCH = 2
SEQ = 16384
DEPTH = 4

GRID_W = 64
CTX_LEN = 256
N_MIXERS = 2
N_LAYERS_NA = (DEPTH + N_MIXERS - 1) // N_MIXERS
N_LAYERS_LRU = DEPTH // N_MIXERS
N_SUBLAYERS = 3
N_MOD = 3 * N_SUBLAYERS
FFN_HIDDEN = 256 * ((8 * D_MODEL // 3 + 255) // 256)
NA_HEADS = 16
NA_HEAD_DIM = D_MODEL // NA_HEADS
NA_WIN_ROWS = 8
NA_WIN_COLS = 16
ROPE_BASE = 10000.0
ROPE_PAIRS_PER_AXIS = NA_HEAD_DIM // 4
LRU_WIDTH = 5 * D_MODEL // 4
LRU_BLOCKS = 10
LRU_BLOCK_W = LRU_WIDTH // LRU_BLOCKS
LRU_C = 8.0
CONV_W = 4
CONV_PAD_LEFT = 2
CONV_PAD_RIGHT = CONV_W - 1 - CONV_PAD_LEFT
RMS_EPS = 1e-6
FFN_RES_WEIGHT = 0.5

kernel_name = 'hybrid_natten_rglru_macaron_dit'


def rms_norm(x, gain):
    x32 = x.astype(jnp.float32)
    y = x32 * lax.rsqrt(jnp.mean(x32 * x32, axis=-1, keepdims=True) + RMS_EPS)
    return (y * gain.astype(jnp.float32)).astype(x.dtype)


def modulate(u, gain, mod, s):
    return rms_norm(u, gain) * (1.0 + mod[:, 3 * s + 1]) + mod[:, 3 * s]


def gated_residual(u, y, gain, mod, s, weight):
    return u + weight * mod[:, 3 * s + 2] * rms_norm(y, gain)


def swiglu(h, w_gate, w_up, w_down):
    return (jax.nn.silu(h @ w_gate) * (h @ w_up)) @ w_down


def axial_rope_tables(n_tok):
    t = jnp.arange(n_tok, dtype=jnp.int32)
    row = (t // GRID_W).astype(jnp.float32)
    col = (t % GRID_W).astype(jnp.float32)
    inv_freq = jnp.power(ROPE_BASE, -jnp.arange(ROPE_PAIRS_PER_AXIS, dtype=jnp.float32) / ROPE_PAIRS_PER_AXIS)
    ang = jnp.concatenate([row[:, None] * inv_freq, col[:, None] * inv_freq], axis=-1)
    return jnp.cos(ang), jnp.sin(ang)


def apply_rope(x, cos, sin):
    x1, x2 = jnp.split(x.astype(jnp.float32), 2, axis=-1)
    cs, sn = cos[None, :, None, :], sin[None, :, None, :]
    return jnp.concatenate([x1 * cs - x2 * sn, x1 * sn + x2 * cs], axis=-1).astype(x.dtype)


def neighbourhood_attention(h, hc, w_qkv, w_o, rpb, cos, sin, need_ctx):
    bsz, n_tok, _ = h.shape
    rows = n_tok // GRID_W
    wr = min(NA_WIN_ROWS, rows)
    wc = min(NA_WIN_COLS, GRID_W)
    scale = NA_HEAD_DIM ** -0.5

    def heads(u):
        return u.reshape(u.shape[0], u.shape[1], NA_HEADS, NA_HEAD_DIM)

    q, k, v = (heads(t) for t in jnp.split(h @ w_qkv, 3, axis=-1))
    q = apply_rope(q, cos, sin)
    k = apply_rope(k, cos, sin)
    kc, vc = (heads(t) for t in jnp.split(hc @ w_qkv[:, D_MODEL:], 2, axis=-1))

    def grid(t):
        return t.reshape(bsz, rows, GRID_W, NA_HEADS, NA_HEAD_DIM)

    q_g, k_g, v_g = grid(q), grid(k), grid(v)

    col = np.arange(GRID_W)
    col_start = np.clip(col - wc // 2, 0, GRID_W - wc)
    col_idx = col_start[:, None] + np.arange(wc)[None, :]
    dc_idx = col_idx - col[:, None] + NA_WIN_COLS - 1
    rpb_cols = rpb[:, :, dc_idx]

    def row_block(r):
        r0 = jnp.clip(r - wr // 2, 0, rows - wr)
        k_win = lax.dynamic_slice_in_dim(k_g, r0, wr, axis=1)[:, :, col_idx]
        v_win = lax.dynamic_slice_in_dim(v_g, r0, wr, axis=1)[:, :, col_idx]
        dr_idx = r0 + jnp.arange(wr, dtype=jnp.int32) - r + NA_WIN_ROWS - 1
        bias = jnp.take(rpb_cols, dr_idx, axis=1).transpose(0, 2, 1, 3)
        q_r = lax.dynamic_index_in_dim(q_g, r, axis=1, keepdims=False)
        s_lat = jnp.einsum('bqhd,bkqjhd->bhqkj', q_r, k_win) * scale + bias[None]
        s_ctx = jnp.einsum('bqhd,bchd->bhqc', q_r, kc) * scale
        s_all = jnp.concatenate([s_lat.reshape(bsz, NA_HEADS, GRID_W, wr * wc), s_ctx], axis=-1)
        p = jax.nn.softmax(s_all.astype(jnp.float32), axis=-1).astype(v.dtype)
        p_lat = p[..., :wr * wc].reshape(bsz, NA_HEADS, GRID_W, wr, wc)
        p_ctx = p[..., wr * wc:]
        return (jnp.einsum('bhqkj,bkqjhd->bqhd', p_lat, v_win)
                + jnp.einsum('bhqc,bchd->bqhd', p_ctx, vc))

    o = lax.map(row_block, jnp.arange(rows, dtype=jnp.int32))
    y = o.transpose(1, 0, 2, 3, 4).reshape(bsz, n_tok, D_MODEL) @ w_o
    if not need_ctx:
        return y, None
    qc = heads(hc @ w_qkv[:, :D_MODEL])
    s_c = jnp.einsum('bqhd,bkhd->bhqk', qc, kc) * scale
    p_c = jax.nn.softmax(s_c.astype(jnp.float32), axis=-1).astype(vc.dtype)
    yc = jnp.einsum('bhqk,bkhd->bqhd', p_c, vc).reshape(bsz, hc.shape[1], D_MODEL) @ w_o
    return y, yc


def depthwise_conv(u, w, b):
    n = u.shape[1]
    up = jnp.pad(u, ((0, 0), (CONV_PAD_LEFT, CONV_PAD_RIGHT), (0, 0)))
    out = up[:, 0:n] * w[0] + b
    for tap in range(1, CONV_W):
        out = out + up[:, tap:tap + n] * w[tap]
    return out


def lru_gates(u, w_a, b_a, w_x, b_x, lam):
    bsz, n, _ = u.shape
    ub = u.reshape(bsz, n, LRU_BLOCKS, LRU_BLOCK_W)
    r = jax.nn.sigmoid(jnp.einsum('blnk,nkj->blnj', ub, w_a).reshape(bsz, n, LRU_WIDTH) + b_a)
    i = jax.nn.sigmoid(jnp.einsum('blnk,nkj->blnj', ub, w_x).reshape(bsz, n, LRU_WIDTH) + b_x)
    log_a = -LRU_C * r.astype(jnp.float32) * jax.nn.softplus(-lam.astype(jnp.float32))
    a = jnp.exp(log_a)
    b = jnp.sqrt(-jnp.expm1(2.0 * log_a)) * (i * u).astype(jnp.float32)
    return a, b


def linear_scan(a, b, reverse):
    def combine(e1, e2):
        a1, b1 = e1
        a2, b2 = e2
        return a1 * a2, a2 * b1 + b2
    return lax.associative_scan(combine, (a, b), axis=1, reverse=reverse)[1]


def rglru_mixer(h, hc, w_in, conv_w, conv_b, w_a, b_a, w_x, b_x, lam, w_o, need_ctx):
    w_gate, w_rec = w_in[:, :LRU_WIDTH], w_in[:, LRU_WIDTH:]
    u = depthwise_conv(h @ w_rec, conv_w, conv_b)
    uc = depthwise_conv(hc @ w_rec, conv_w, conv_b)
    lat_states, ctx_states = [], []
    for d, reverse in enumerate((False, True)):
        ctx_edge = 0 if reverse else -1
        lat_edge = -1 if reverse else 0
        a_c, b_c = lru_gates(uc, w_a[d], b_a[d], w_x[d], b_x[d], lam[d])
        hs_c = linear_scan(a_c, b_c, reverse)
        a, b = lru_gates(u, w_a[d], b_a[d], w_x[d], b_x[d], lam[d])
        b = b.at[:, lat_edge].add(a[:, lat_edge] * hs_c[:, ctx_edge])
        lat_states.append(linear_scan(a, b, reverse))
        ctx_states.append(hs_c)
    rec = (lat_states[0] + lat_states[1]).astype(h.dtype)
    y = (rec * jax.nn.gelu(h @ w_gate)) @ w_o
    if not need_ctx:
        return y, None
    rec_c = (ctx_states[0] + ctx_states[1]).astype(hc.dtype)
    yc = (rec_c * jax.nn.gelu(hc @ w_gate)) @ w_o
    return y, yc


def setup_inputs(seed: int = 0) -> dict:
    key = jax.random.key(seed)
    ks = jax.random.split(key, 24)
    f32 = jnp.float32

    def nrm(k, shape, scale):
        return jax.random.normal(k, shape, f32) * scale

    a_pow = jax.random.uniform(ks[21], (N_LAYERS_LRU, 2, LRU_WIDTH), f32, 0.9, 0.999)
    s = a_pow ** (1.0 / LRU_C)
    return {
        'x': nrm(ks[0], (BATCH, SEQ, D_MODEL), 1.0),
        'c': nrm(ks[1], (BATCH, D_MODEL), 1.0),
        'ctx': nrm(ks[2], (BATCH, CTX_LEN, D_MODEL), 1.0),
        'c_ctx': nrm(ks[3], (D_MODEL,), 1.0),
        'ada_w': nrm(ks[4], (DEPTH, D_MODEL, N_MOD * D_MODEL), D_MODEL ** -0.5),
        'ada_b': nrm(ks[5], (DEPTH, N_MOD * D_MODEL), 0.01),
        'norm_pre': 1.0 + nrm(ks[6], (DEPTH, N_SUBLAYERS, D_MODEL), 0.05),
        'norm_post': 1.0 + nrm(ks[7], (DEPTH, N_SUBLAYERS, D_MODEL), 0.05),
        'ffn_w_gate': nrm(ks[8], (DEPTH, 2, D_MODEL, FFN_HIDDEN), D_MODEL ** -0.5),
        'ffn_w_up': nrm(ks[9], (DEPTH, 2, D_MODEL, FFN_HIDDEN), D_MODEL ** -0.5),
        'ffn_w_down': nrm(ks[10], (DEPTH, 2, FFN_HIDDEN, D_MODEL), FFN_HIDDEN ** -0.5),
        'na_w_qkv': nrm(ks[11], (N_LAYERS_NA, D_MODEL, 3 * D_MODEL), D_MODEL ** -0.5),
        'na_w_o': nrm(ks[12], (N_LAYERS_NA, D_MODEL, D_MODEL), D_MODEL ** -0.5),
        'na_rpb': nrm(ks[13], (N_LAYERS_NA, NA_HEADS, 2 * NA_WIN_ROWS - 1, 2 * NA_WIN_COLS - 1), 0.1),
        'lru_w_in': nrm(ks[14], (N_LAYERS_LRU, D_MODEL, 2 * LRU_WIDTH), D_MODEL ** -0.5),
        'lru_conv_w': nrm(ks[15], (N_LAYERS_LRU, CONV_W, LRU_WIDTH), CONV_W ** -0.5),
        'lru_conv_b': nrm(ks[16], (N_LAYERS_LRU, LRU_WIDTH), 0.01),
        'lru_w_a': nrm(ks[17], (N_LAYERS_LRU, 2, LRU_BLOCKS, LRU_BLOCK_W, LRU_BLOCK_W), LRU_BLOCK_W ** -0.5),
        'lru_b_a': nrm(ks[18], (N_LAYERS_LRU, 2, LRU_WIDTH), 0.01),
        'lru_w_x': nrm(ks[19], (N_LAYERS_LRU, 2, LRU_BLOCKS, LRU_BLOCK_W, LRU_BLOCK_W), LRU_BLOCK_W ** -0.5),
        'lru_b_x': nrm(ks[20], (N_LAYERS_LRU, 2, LRU_WIDTH), 0.01),
        'lru_lambda': jnp.log(s) - jnp.log1p(-s),
        'lru_w_o': nrm(ks[22], (N_LAYERS_LRU, LRU_WIDTH, D_MODEL), LRU_WIDTH ** -0.5),
    }


def reference(x, c, ctx, c_ctx, ada_w, ada_b, norm_pre, norm_post, ffn_w_gate, ffn_w_up, ffn_w_down,
              na_w_qkv, na_w_o, na_rpb, lru_w_in, lru_conv_w, lru_conv_b, lru_w_a, lru_b_a,
              lru_w_x, lru_b_x, lru_lambda, lru_w_o):
    cos, sin = axial_rope_tables(x.shape[1])
    xc = ctx
    for i in range(DEPTH):
        last = i == DEPTH - 1
        j = i // N_MIXERS
        mod = (jax.nn.silu(c) @ ada_w[i] + ada_b[i]).reshape(x.shape[0], N_MOD, 1, D_MODEL)
        mod_c = (jax.nn.silu(c_ctx) @ ada_w[i] + ada_b[i]).reshape(1, N_MOD, 1, D_MODEL)

        def half_ffn(u, m, s):
            f = s // 2
            y = swiglu(modulate(u, norm_pre[i, s], m, s), ffn_w_gate[i, f], ffn_w_up[i, f], ffn_w_down[i, f])
            return gated_residual(u, y, norm_post[i, s], m, s, FFN_RES_WEIGHT)

        x = half_ffn(x, mod, 0)
        xc = half_ffn(xc, mod_c, 0)
        h = modulate(x, norm_pre[i, 1], mod, 1)
        hc = modulate(xc, norm_pre[i, 1], mod_c, 1)
        if i % N_MIXERS == 0:
            y, yc = neighbourhood_attention(h, hc, na_w_qkv[j], na_w_o[j], na_rpb[j], cos, sin, not last)
        else:
            y, yc = rglru_mixer(h, hc, lru_w_in[j], lru_conv_w[j], lru_conv_b[j], lru_w_a[j], lru_b_a[j],
                                lru_w_x[j], lru_b_x[j], lru_lambda[j], lru_w_o[j], not last)
        x = gated_residual(x, y, norm_post[i, 1], mod, 1, 1.0)
        x = half_ffn(x, mod, 2)
        if not last:
            xc = gated_residual(xc, yc, norm_post[i, 1], mod_c, 1, 1.0)
            xc = half_ffn(xc, mod_c, 2)
    return x
```

```python
import os
import numpy as np
from contextlib import ExitStack
import concourse.bass as bass
import concourse.mybir as mybir
from concourse.bass_utils import run_bass_kernel_spmd

F32 = mybir.dt.float32
BF16 = mybir.dt.bfloat16
AF = mybir.ActivationFunctionType
ALU = mybir.AluOpType

D = 1024
KC = 8
HID = 2816
NH = 22
CTX = 256
LW = 1280
LC = 10
NEG = -30000.0
EPS = 1e-6
ENGS = ("pe", "act", "dve", "pool", "sp")
SAME_ENG_SYNC = True
ATT_PIPE = True


class Op:
    __slots__ = ("eng", "fn", "deps", "signals", "sigval", "dma", "dslot", "dval")


class Prog:
    def __init__(self, nc):
        self.nc = nc
        self.ops = {e: [] for e in ENGS}
        self.last_w = {}
        self.readers = {}
        self.dma_n = {e: 0 for e in ENGS}
        self.dma_slots = {"sp": 16, "pool": 16, "act": 4}
        self.dma_hist = {e: [] for e in ENGS}

    def op(self, eng, fn, reads=(), writes=(), dma=False, extra=()):
        o = Op()
        o.eng, o.fn, o.signals, o.sigval, o.dma, o.dslot, o.dval = eng, fn, False, None, dma, None, None
        deps = list(extra)
        for k in reads:
            w = self.last_w.get(k)
            if w is not None:
                deps.append(w)
        for k in writes:
            w = self.last_w.get(k)
            if w is not None:
                deps.append(w)
            deps.extend(self.readers.get(k, ()))
        for k in reads:
            self.readers.setdefault(k, []).append(o)
        for k in writes:
            self.last_w[k] = o
            self.readers[k] = []
        if dma:
            n = self.dma_n[eng]
            ns = self.dma_slots[eng]
            o.dslot = n % ns
            o.dval = 16 * (n // ns + 1)
            if n >= ns:
                deps.append(self.dma_hist[eng][n - ns])
            self.dma_hist[eng].append(o)
            self.dma_n[eng] = n + 1
        o.deps = []
        seen = set()
        for d in deps:
            if d is o or id(d) in seen:
                continue
            seen.add(id(d))
            if (not d.dma) and d.eng == eng and (eng == "pe" or not SAME_ENG_SYNC):
                continue
            o.deps.append(d)
            if not d.dma:
                d.signals = True
        self.ops[eng].append(o)
        return o

    def dma(self, queue, out, in_, reads=(), writes=(), **kw):
        return self.op(queue, lambda e: e.dma_start(out=out, in_=in_, **kw), reads=reads, writes=writes, dma=True)

    def barrier(self):
        lasts = [self.ops[e][-1] for e in ENGS if self.ops[e] and self.ops[e][-1].fn is not None]
        for e in ENGS:
            for o in reversed(self.ops[e]):
                if o.fn is not None and o not in lasts:
                    lasts.append(o)
                    break
        dmas = []
        for q, ns in self.dma_slots.items():
            dmas.extend(self.dma_hist[q][-ns:])
        for e in ENGS:
            self.op(e, None, extra=lasts + dmas)
        self.last_w = {}
        self.readers = {}

    def emit(self, final_wait_ops=()):
        nc = self.nc
        with ExitStack() as st:
            sems = {e: st.enter_context(nc.semaphore(f"sem_{e}")) for e in ENGS}
            dsems = {q: [st.enter_context(nc.semaphore(f"dsem_{q}{i}")) for i in range(ns)]
                     for q, ns in self.dma_slots.items()}
            for e in ENGS:
                c = 0
                for o in self.ops[e]:
                    if o.signals and not o.dma:
                        c += 1
                        o.sigval = c
            block = st.enter_context(nc.Block())

            self.evlog = {e: [] for e in ENGS}

            def run(ename, eng):
                seen_c = {e: 0 for e in ENGS}
                seen_d = {}
                log = self.evlog[ename]
                for o in self.ops[ename]:
                    need_c, need_d = {}, {}
                    for d in o.deps:
                        if d.dma:
                            k = (d.eng, d.dslot)
                            if seen_d.get(k, 0) < d.dval and need_d.get(k, 0) < d.dval:
                                need_d[k] = d.dval
                        elif seen_c[d.eng] < d.sigval and need_c.get(d.eng, 0) < d.sigval:
                            need_c[d.eng] = d.sigval
                    for f, v in need_c.items():
                        eng.wait_ge(sems[f], v)
                        seen_c[f] = v
                        log.append(("w", f, v))
                    for (q, s), v in need_d.items():
                        eng.wait_ge(dsems[q][s], v)
                        seen_d[(q, s)] = v
                        log.append(("w", (q, s), v))
                    if o.fn is None:
                        continue
                    ins = o.fn(eng)
                    if o.dma:
                        ins.then_inc(dsems[o.eng][o.dslot], 16)
                        log.append(("i", (o.eng, o.dslot), 16))
                    elif o.signals:
                        ins.then_inc(sems[ename], 1)
                        log.append(("i", ename, 1))
                if ename == "sp":
                    for o in final_wait_ops:
                        eng.wait_ge(dsems[o.eng][o.dslot], o.dval)

            block.tensor(lambda eng: run("pe", eng))
            block.scalar(lambda eng: run("act", eng))
            block.vector(lambda eng: run("dve", eng))
            block.gpsimd(lambda eng: run("pool", eng))
            block.sync(lambda eng: run("sp", eng))


class Arena:
    def __init__(self, raw, nwords):
        self.raw, self.n, self.off, self.cnt = raw, nwords, 0, 0

    def _take(self, words):
        words = (words + 7) // 8 * 8
        a = self.off
        self.off += words
        assert self.off <= self.n, f"SBUF arena overflow {self.off}>{self.n}"
        return a

    def f32(self, free):
        a = self._take(free)
        return self.raw[:, a:a + free]

    def bf16(self, free):
        w = (free + 1) // 2
        a = self._take(w)
        return self.raw[:, a:a + w].bitcast(BF16)[:, 0:free]

    def mark(self):
        return self.off

    def reset(self, m):
        self.off = m


def pair_chunks(Pq, NR):
    r0a = min(max(2 * Pq - 4, 0), NR - 8)
    r0b = min(max(2 * Pq + 1 - 4, 0), NR - 8)
    return list(range(r0a // 2, (r0b + 7) // 2 + 1))


def bias_set_of(Pq, NP):
    if Pq == 0:
        return 1
    if Pq == 1:
        return 2
    if Pq == NP - 2:
        return 3
    if Pq == NP - 1:
        return 4
    return 0


def build(S, layers=(0, 1, 2, 3)):
    NG = S // 512
    NR = S // 64
    NP = NR // 2
    nc = bass.Bass("TRN2", target_bir_lowering=False)

    def din(name, shape):
        return nc.dram_tensor(name, list(shape), F32, kind="ExternalInput").ap()

    x_in = din("x", [S, D])
    ctx_in = din("ctx", [CTX, D])
    cv_in = din("cvec", [2, D])
    ada_w = din("ada_w", [4, D, 9 * D])
    ada_b = din("ada_b", [4, 9 * D])
    norm_pre = din("norm_pre", [4, 3, D])
    norm_post = din("norm_post", [4, 3, D])
    w_gate = din("ffn_w_gate", [4, 2, D, HID])
    w_up = din("ffn_w_up", [4, 2, D, HID])
    w_down = din("ffn_w_down", [4, 2, HID, D])
    na_w = din("na_w_ext", [2, D, 5 * D])
    na_wo = din("na_w_o", [2, D, D])
    na_bias = din("na_bias", [2, 5, 128, 16 * 5 * 128])
    l_win = din("lru_w_in", [2, D, 2 * LW])
    l_cw = din("lru_conv_w", [2, 4, LW])
    l_cb = din("lru_conv_b", [2, LW])
    l_wa = din("lru_w_a", [2, 2, LC, 128, 128])
    l_ba = din("lru_b_a", [2, 2, LW])
    l_wx = din("lru_w_x", [2, 2, LC, 128, 128])
    l_bx = din("lru_b_x", [2, 2, LW])
    l_lam = din("lru_lambda", [2, 2, LW])
    l_wo = din("lru_w_o", [2, LW, D])
    ropeC = din("ropeC", [128, S])
    ropeS = din("ropeS", [128, S])
    ident_in = din("ident", [128, 128])
    out = nc.dram_tensor("out", [S, D], F32, kind="ExternalOutput").ap()

    def scratch(name, shape, dt=F32):
        return nc.dram_tensor(name, list(shape), dt, kind="Internal").ap()

    xs = scratch("xs", [S, D])
    xcs = scratch("xcs", [CTX, D])
    modbuf = scratch("modbuf", [2, 9 * D])
    qTd = scratch("qTd", [KC, 128, S], BF16)
    kTd = scratch("kTd", [KC, 128, S], BF16)
    vvd = scratch("vvd", [S, D], BF16)
    upre = scratch("upre", [LC, 128, S])
    ubuf = scratch("ubuf", [LC, 128, S])
    ggd = scratch("ggd", [LC, 128, S], BF16)
    hfd = scratch("hfd", [LC, 128, S])

    P = Prog(nc)
    NW = 53000
    RAW = nc.alloc_sbuf_tensor("raw", [128, NW], F32)
    A = Arena(RAW, NW)
    PS = nc.alloc_psum_tensor("ps", [128, 8, 512], F32)
    pT = PS[:, 7, :].bitcast(BF16)

    def psk(b):
        return ("ps", b)

    ident = A.bf16(128)
    ones64 = A.bf16(64)
    s_bf = A.bf16(KC * 2).rearrange("p (k v) -> p k v", v=2)
    gs = A.f32(D)
    sh = A.f32(D)
    cf = A.f32(D)
    tmpf = A.f32(D)
    ss = A.f32(8)
    rs = A.f32(8)
    P.dma("pool", ident, ident_in, writes=["ident"])
    P.op("pool", lambda e: e.memset(ones64, 1.0), writes=["ones64"])
    cT = A.f32(KC * 2).rearrange("p (v k) -> p v k", v=2)
    for v in range(2):
        P.dma("sp", cT[:, v, :], cv_in[v, :].rearrange("(k p) -> p k", p=128), writes=[("cT", v)],
              allow_slow_non_contiguous=True)
    P.op("act", lambda e: e.activation(out=s_bf, in_=cT.rearrange("p v k -> p k v"), func=AF.Silu),
         reads=[("cT", 0), ("cT", 1)], writes=["s_bf"])
    base_mark = A.mark()

    def load_modtiles(L, s, v, weight):
        def bc(dst, src, key):
            P.dma("sp", dst, src.partition_broadcast(128), reads=["modbuf"], writes=[key])
        bc(gs, modbuf[v, (3 * s + 1) * D:(3 * s + 2) * D], "gs")
        bc(tmpf, norm_pre[L, s, :], "tmpf")
        P.op("dve", lambda e: e.scalar_tensor_tensor(out=gs, in0=gs, scalar=1.0, in1=tmpf, op0=ALU.add, op1=ALU.mult),
             reads=["gs", "tmpf"], writes=["gs"])
        bc(sh, modbuf[v, (3 * s) * D:(3 * s + 1) * D], "sh")
        bc(cf, modbuf[v, (3 * s + 2) * D:(3 * s + 3) * D], "cf")
        bc(tmpf, norm_post[L, s, :], "tmpf")
        P.op("dve", lambda e: e.scalar_tensor_tensor(out=cf, in0=cf, scalar=float(weight), in1=tmpf,
                                                      op0=ALU.mult, op1=ALU.mult),
             reads=["cf", "tmpf"], writes=["cf"])

    def rstd_from(ssap, n, rsap, ksrc, kdst):
        P.op("dve", lambda e: e.tensor_scalar(out=rsap[:, 0:n], in0=ssap[:, 0:n], scalar1=1.0 / D, scalar2=EPS,
                                               op0=ALU.mult, op1=ALU.add), reads=ksrc, writes=[kdst])
        P.op("act", lambda e: e.activation(out=rsap[:, 0:n], in_=rsap[:, 0:n], func=AF.Sqrt), reads=[kdst], writes=[kdst])
        P.op("dve", lambda e: e.reciprocal(out=rsap[:, 0:n], in_=rsap[:, 0:n]), reads=[kdst], writes=[kdst])

    def prep_h(xt, xkey, nt, hb, hT, hTkey):
        ssk = [("ss", t) for t in range(nt)]
        P.op("pool", lambda e: e.memset(ss[:, 0:nt], 0.0), writes=ssk)
        for t in range(nt):
            P.op("act", lambda e, t=t: e.activation(out=hb, in_=xt[:, t, :], func=AF.Square, accum_out=ss[:, t:t + 1]),
                 reads=[xkey], writes=[("ss", t), "hb"])
        rstd_from(ss, nt, rs, ssk, "rs")
        for t in range(nt):
            P.op("dve", lambda e, t=t: e.scalar_tensor_tensor(out=tmpf, in0=xt[:, t, :], scalar=rs[:, t:t + 1], in1=gs,
                                                               op0=ALU.mult, op1=ALU.mult),
                 reads=[xkey, "rs", "gs"], writes=["tmpf"])
            P.op("pool", lambda e: e.tensor_tensor(out=hb, in0=tmpf, in1=sh, op=ALU.add),
                 reads=["tmpf", "sh"], writes=["hb"])

            def tr(e):
                for k in range(KC):
                    r = e.transpose(out=pT[:, k * 128:(k + 1) * 128], in_=hb[:, k * 128:(k + 1) * 128], identity=ident)
                return r
            P.op("pe", tr, reads=["hb", "ident"], writes=[psk(7)])
            P.op("act", lambda e, t=t: e.copy(out=hT[:, :, t * 128:(t + 1) * 128],
                                              in_=pT.rearrange("p (k n) -> p k n", k=KC)),
                 reads=[psk(7)], writes=[hTkey])

    ybank = [0]

    junk_key = [None]

    def post_resid(ymm, xt_tile, xkey, extra_reads):
        b0 = 4 + ybank[0] % 3
        b1 = 4 + (ybank[0] + 1) % 3
        ybank[0] += 2
        for half, b in ((0, b0), (1, b1)):
            P.op("pe", lambda e, half=half, b=b: ymm(e, half, PS[:, b, :]), reads=extra_reads, writes=[psk(b)])
        P.op("pool", lambda e: e.memset(ss[:, 4:6], 0.0), writes=[("ss2", 0), ("ss2", 1)])
        for half, b in ((0, b0), (1, b1)):
            P.op("act", lambda e, half=half, b=b, jk=junk: e.activation(out=jk[:, 0:512], in_=PS[:, b, :], func=AF.Square,
                                                                         accum_out=ss[:, 4 + half:5 + half]),
                 reads=[psk(b)], writes=[("ss2", half)] + ([junk_key[0]] if junk_key[0] else []))
        P.op("dve", lambda e: e.tensor_scalar(out=rs[:, 4:5], in0=ss[:, 4:5], scalar1=ss[:, 5:6], scalar2=1.0 / D,
                                               op0=ALU.add, op1=ALU.mult), reads=[("ss2", 0), ("ss2", 1)], writes=["rs2"])
        P.op("act", lambda e: e.activation(out=rs[:, 4:5], in_=rs[:, 4:5], func=AF.Sqrt, bias=EPS),
             reads=["rs2"], writes=["rs2"])
        P.op("dve", lambda e: e.reciprocal(out=rs[:, 4:5], in_=rs[:, 4:5]), reads=["rs2"], writes=["rs2"])
        for half, b in ((0, b0), (1, b1)):
            P.op("dve", lambda e, half=half, b=b: e.scalar_tensor_tensor(
                out=tmpf[:, half * 512:(half + 1) * 512], in0=PS[:, b, :], scalar=rs[:, 4:5],
                in1=cf[:, half * 512:(half + 1) * 512], op0=ALU.mult, op1=ALU.mult),
                 reads=[psk(b), "rs2", "cf"], writes=["tmpf"])
        P.op("pool", lambda e: e.tensor_tensor(out=xt_tile, in0=xt_tile, in1=tmpf, op=ALU.add),
             reads=["tmpf", xkey], writes=[xkey])

    def xload(dst, g, nt, src, key):
        P.dma("sp", dst, src[g * nt * 128:(g + 1) * nt * 128, :].rearrange("(t p) d -> p t d", p=128),
              reads=[("xd", src.tensor.name, g)], writes=[key])

    def xstore(srct, g, nt, dst, key):
        return P.dma("sp", dst[g * nt * 128:(g + 1) * nt * 128, :].rearrange("(t p) d -> p t d", p=128), srct,
                     reads=[key], writes=[("xd", dst.tensor.name, g)])

    final_stores = []

    def emit_mod(L):
        m = A.mark()
        awt = [A.bf16(KC * 512).rearrange("p (k n) -> p k n", k=KC) for _ in range(2)]
        modsb = A.f32(9 * D)
        adab = A.f32(9 * D)
        P.dma("sp", adab[0:2, :], ada_b[L, :].partition_broadcast(2), writes=["adab"])
        for ct in range(18):
            b = ct % 2
            P.dma("pool", awt[b], ada_w[L, :, ct * 512:(ct + 1) * 512].rearrange("(k p) n -> p k n", p=128),
                  writes=[("awt", b)])

            def mm(e, b=b):
                for k in range(KC):
                    r = e.matmul(PS[0:2, b, :], lhsT=s_bf[:, k, :], rhs=awt[b][:, k, :], start=(k == 0), stop=(k == KC - 1))
                return r
            P.op("pe", mm, reads=[("awt", b), "s_bf"], writes=[psk(b)])
            P.op("dve", lambda e, b=b, ct=ct: e.tensor_tensor(out=modsb[0:2, ct * 512:(ct + 1) * 512], in0=PS[0:2, b, :],
                                                               in1=adab[0:2, ct * 512:(ct + 1) * 512], op=ALU.add),
                 reads=[psk(b), "adab"], writes=["modsb"])
        P.dma("sp", modbuf, modsb[0:2, :], reads=["modsb"], writes=["modbuf"])
        P.barrier()
        A.reset(m)

    def load_mod_pre(L, s, v):
        def bc(dst, src, key):
            P.dma("sp", dst, src.partition_broadcast(128), reads=["modbuf"], writes=[key])
        bc(gs, modbuf[v, (3 * s + 1) * D:(3 * s + 2) * D], "gs")
        bc(tmpf, norm_pre[L, s, :], "tmpf")
        P.op("dve", lambda e: e.scalar_tensor_tensor(out=gs, in0=gs, scalar=1.0, in1=tmpf, op0=ALU.add, op1=ALU.mult),
             reads=["gs", "tmpf"], writes=["gs"])
        bc(sh, modbuf[v, (3 * s) * D:(3 * s + 1) * D], "sh")

    def load_mod_post(L, s, v, weight):
        def bc(dst, src, key):
            P.dma("sp", dst, src.partition_broadcast(128), reads=["modbuf"], writes=[key])
        bc(cf, modbuf[v, (3 * s + 2) * D:(3 * s + 3) * D], "cf")
        bc(tmpf, norm_post[L, s, :], "tmpf")
        P.op("dve", lambda e: e.scalar_tensor_tensor(out=cf, in0=cf, scalar=float(weight), in1=tmpf,
                                                      op0=ALU.mult, op1=ALU.mult),
             reads=["cf", "tmpf"], writes=["cf"])

    def emit_ffn(L, s, xsrc, xdst, csrc, do_ctx, is_final):
        nonlocal junk
        f = s // 2
        m = A.mark()
        Wg = A.bf16(KC * HID).rearrange("p (k n) -> p k n", k=KC)
        Wu = A.bf16(KC * HID).rearrange("p (k n) -> p k n", k=KC)
        Wd = A.bf16(NH * D).rearrange("p (j n) -> p j n", j=NH)
        hb = A.bf16(D)
        hT = A.bf16(KC * 512).rearrange("p (k n) -> p k n", k=KC)
        G = A.bf16(NH * 512).rearrange("p (j n) -> p j n", j=NH)
        sg = A.f32(512) if os.environ.get("MK_SGF32", "0") == "1" else A.bf16(512)
        XA = A.f32(4 * D).rearrange("p (t d) -> p t d", t=4)
        XB = [A.f32(D) for _ in range(2)]
        junk = sg.bitcast(BF16) if os.environ.get("MK_SGF32", "0") == "1" else sg
        junk_key[0] = "sg"
        for k in range(KC):
            P.dma("pool", Wg[:, k, :], w_gate[L, f, k * 128:(k + 1) * 128, :], writes=[("wg", k)])
            P.dma("pool", Wu[:, k, :], w_up[L, f, k * 128:(k + 1) * 128, :], writes=[("wu", k)])
        for j in range(NH):
            P.dma("pool", Wd[:, j, :], w_down[L, f, j * 128:(j + 1) * 128, :], writes=[("wd", j)])
        wgk = [("wg", k) for k in range(KC)]
        wuk = [("wu", k) for k in range(KC)]
        wdk = [("wd", j) for j in range(NH)]
        gk = [("G", j) for j in range(NH)]
        work = [(g, 4, xsrc, xdst, 0, is_final) for g in range(NG)]
        if do_ctx:
            work.append((0, 2, csrc, xcs, 1, False))
        xbi = [0]

        def stage_prep(w):
            g, nt, src, dst, v, final = w
            xload(XA[:, 0:nt, :], g, nt, src, "XA")
            prep_h(XA, "XA", nt, hb, hT, "hT")

        def stage_gateup(w):
            g, nt, src, dst, v, final = w
            n = nt * 128
            for j in range(NH):
                ba, bb = j % 2, 2 + j % 2

                def mmg(e, j=j, ba=ba, W=Wg):
                    for k in range(KC):
                        r = e.matmul(PS[:, ba, 0:n], lhsT=W[:, k, j * 128:(j + 1) * 128], rhs=hT[:, k, 0:n],
                                     start=(k == 0), stop=(k == KC - 1))
                    return r

                def mmu(e, j=j, bb=bb, W=Wu):
                    for k in range(KC):
                        r = e.matmul(PS[:, bb, 0:n], lhsT=W[:, k, j * 128:(j + 1) * 128], rhs=hT[:, k, 0:n],
                                     start=(k == 0), stop=(k == KC - 1))
                    return r
                P.op("pe", mmg, reads=wgk + ["hT"], writes=[psk(ba)])
                P.op("pe", mmu, reads=wuk + ["hT"], writes=[psk(bb)])
                P.op("act", lambda e, ba=ba: e.activation(out=sg[:, 0:n], in_=PS[:, ba, 0:n], func=AF.Silu),
                     reads=[psk(ba)], writes=["sg"])
                P.op("dve", lambda e, j=j, bb=bb: e.tensor_tensor(out=G[:, j, 0:n], in0=sg[:, 0:n], in1=PS[:, bb, 0:n],
                                                                   op=ALU.mult),
                     reads=["sg", psk(bb)], writes=[("G", j)])

        def stage_down(w):
            g, nt, src, dst, v, final = w
            for t in range(nt):
                xb = XB[xbi[0] % 2]
                xkey = ("XB", xbi[0] % 2)
                xbi[0] += 1
                r0 = g * nt * 128 + t * 128
                P.dma("sp", xb, src[r0:r0 + 128, :], reads=[("xd", src.tensor.name, g)], writes=[xkey])

                def ymm(e, half, outap, t=t):
                    for j in range(NH):
                        r = e.matmul(outap, lhsT=G[:, j, t * 128:(t + 1) * 128], rhs=Wd[:, j, half * 512:(half + 1) * 512],
                                     start=(j == 0), stop=(j == NH - 1))
                    return r
                post_resid(ymm, xb, xkey, gk + wdk)
                st = P.dma("sp", dst[r0:r0 + 128, :], xb, reads=[xkey], writes=[("xd", dst.tensor.name, g, t)])
                if final:
                    final_stores.append(st)

        cur_v = None
        load_mod_pre(L, s, 0)
        load_mod_post(L, s, 0, 0.5)
        stage_prep(work[0])
        for wi, w in enumerate(work):
            stage_gateup(w)
            if wi + 1 < len(work):
                nw = work[wi + 1]
                if nw[4] != w[4]:
                    load_mod_pre(L, s, nw[4])
                stage_prep(nw)
            stage_down(w)
            if wi + 1 < len(work) and work[wi + 1][4] != w[4]:
                load_mod_post(L, s, work[wi + 1][4], 0.5)
        junk_key[0] = None
        P.barrier()
        A.reset(m)

    junk = None

    def emit_na(L, xsrc, xdst):
        nonlocal junk
        j = L // 2
        m0 = A.mark()
        qcT = A.bf16(KC * 256).rearrange("p (k n) -> p k n", k=KC)
        kcT = A.bf16(KC * 256).rearrange("p (k n) -> p k n", k=KC)
        vc = A.bf16(2 * D).rearrange("p (t d) -> p t d", t=2)
        m = A.mark()
        Wq = A.bf16(KC * 5 * D).rearrange("p (k n) -> p k n", k=KC)
        hb = A.bf16(D)
        junk = A.bf16(1024)
        hT = A.bf16(KC * 512).rearrange("p (k n) -> p k n", k=KC)
        xt = A.f32(4 * D).rearrange("p (t d) -> p t d", t=4)
        rC = A.f32(512)
        rS = A.f32(512)
        t1 = A.f32(512)
        t2 = A.f32(512)
        qst = A.bf16(KC * 512).rearrange("p (k n) -> p k n", k=KC)
        kst = A.bf16(KC * 512).rearrange("p (k n) -> p k n", k=KC)
        vst = A.bf16(4 * D).rearrange("p (t d) -> p t d", t=4)
        for k in range(KC):
            P.dma("pool", Wq[:, k, :], na_w[j, k * 128:(k + 1) * 128, :], writes=[("wq", k)])
        wqk = [("wq", k) for k in range(KC)]

        def proj(colbase, oc, n, bank):
            def mm(e):
                for k in range(KC):
                    r = e.matmul(PS[:, bank, 0:n], lhsT=Wq[:, k, colbase + oc * 128:colbase + (oc + 1) * 128],
                                 rhs=hT[:, k, 0:n], start=(k == 0), stop=(k == KC - 1))
                return r
            P.op("pe", mm, reads=wqk + ["hT"], writes=[psk(bank)])

        def vproj(nt, dstv, dkey):
            for t in range(nt):
                for half in range(2):
                    b = 4 + (2 * t + half) % 3

                    def mm(e, t=t, half=half, b=b):
                        for k in range(KC):
                            r = e.matmul(PS[:, b, :], lhsT=hT[:, k, t * 128:(t + 1) * 128],
                                         rhs=Wq[:, k, 4 * D + half * 512:4 * D + (half + 1) * 512],
                                         start=(k == 0), stop=(k == KC - 1))
                        return r
                    P.op("pe", mm, reads=wqk + ["hT"], writes=[psk(b)])
                    P.op("act", lambda e, t=t, half=half, b=b: e.copy(out=dstv[:, t, half * 512:(half + 1) * 512],
                                                                     in_=PS[:, b, :]),
                         reads=[psk(b)], writes=[dkey])

        load_modtiles(L, 1, 1, 1.0)
        xload(xt[:, 0:2, :], 0, 2, xcs, "xt")
        prep_h(xt, "xt", 2, hb, hT, "hT")
        for (cb, dstT, dkey, scl) in ((0, qcT, "qcT", 0.125), (2 * D, kcT, "kcT", 1.0)):
            for oc in range(KC):
                ba = oc % 2
                proj(cb, oc, 256, ba)
                P.op("act", lambda e, dstT=dstT, oc=oc, ba=ba, scl=scl: e.mul(out=dstT[:, oc, :], in_=PS[:, ba, 0:256], mul=scl),
                     reads=[psk(ba)], writes=[dkey])
        vproj(2, vc, "vc")
        load_modtiles(L, 1, 0, 1.0)
        for g in range(NG):
            xload(xt, g, 4, xsrc, "xt")
            prep_h(xt, "xt", 4, hb, hT, "hT")
            P.dma("sp", rC, ropeC[:, g * 512:(g + 1) * 512], writes=["rC"])
            P.dma("sp", rS, ropeS[:, g * 512:(g + 1) * 512], writes=["rS"])
            for (cb, stg, skey, scl) in ((0, qst, "qst", 0.125), (2 * D, kst, "kst", 1.0)):
                for oc in range(KC):
                    ba, bb = oc % 2, 2 + oc % 2
                    proj(cb, oc, 512, ba)
                    proj(cb + D, oc, 512, bb)
                    P.op("dve", lambda e, ba=ba, scl=scl: e.scalar_tensor_tensor(out=t1, in0=PS[:, ba, :], scalar=scl, in1=rC,
                                                                                  op0=ALU.mult, op1=ALU.mult),
                         reads=[psk(ba), "rC"], writes=["t1"])
                    P.op("dve", lambda e, bb=bb, scl=scl: e.scalar_tensor_tensor(out=t2, in0=PS[:, bb, :], scalar=scl, in1=rS,
                                                                                  op0=ALU.mult, op1=ALU.mult),
                         reads=[psk(bb), "rS"], writes=["t2"])
                    P.op("pool", lambda e, stg=stg, oc=oc: e.tensor_tensor(out=stg[:, oc, :], in0=t1, in1=t2, op=ALU.add),
                         reads=["t1", "t2"], writes=[skey])
            P.dma("sp", qTd[:, :, g * 512:(g + 1) * 512].rearrange("k p n -> p k n"), qst, reads=["qst"],
                  writes=[("qTd", g)])
            P.dma("sp", kTd[:, :, g * 512:(g + 1) * 512].rearrange("k p n -> p k n"), kst, reads=["kst"],
                  writes=[("kTd", g)])
            vproj(4, vst, "vst")
            P.dma("sp", vvd[g * 512:(g + 1) * 512, :].rearrange("(t p) d -> p t d", p=128), vst, reads=["vst"],
                  writes=[("vvd", g)])
        P.barrier()
        A.reset(m)

        m = A.mark()
        Wo = A.bf16(KC * D).rearrange("p (k n) -> p k n", k=KC)
        hb = A.bf16(D)
        junk = A.bf16(1024)
        xt = A.f32(4 * D).rearrange("p (t d) -> p t d", t=4)
        Kw = A.bf16(KC * 1024).rearrange("p (k n) -> p k n", k=KC)
        Vw = A.bf16(8 * D).rearrange("p (c d) -> p c d", c=8)
        qb = A.bf16(KC * 512).rearrange("p (k n) -> p k n", k=KC)
        oT = A.bf16(KC * 512).rearrange("p (k n) -> p k n", k=KC)
        bI = A.bf16(16 * 5 * 128).rearrange("p (h o q) -> p h o q", h=16, o=5)
        bSp = [A.bf16(16 * 5 * 128).rearrange("p (h o q) -> p h o q", h=16, o=5) for _ in range(2)]
        PT = [A.bf16(8 * 128).rearrange("p (c q) -> p c q", c=8) for _ in range(2)]
        rinv = A.f32(128)
        for k in range(KC):
            P.dma("pool", Wo[:, k, :], na_wo[j, k * 128:(k + 1) * 128, :], writes=[("wo", k)])
        P.dma("pool", bI.rearrange("p h o q -> p (h o q)"), na_bias[j, 0], writes=["bI"])
        wok = [("wo", k) for k in range(KC)]
        load_modtiles(L, 1, 1, 1.0)
        xload(xt[:, 0:2, :], 0, 2, xcs, "xt")

        itc = [0]
        pend = []

        def flush_att():
            while pend:
                pend.pop(0)()

        def attend(chunks, qsel, osel, rkeys, okey):
            for hd in range(16):
                hc, hp = hd // 2, (hd % 2) * 64
                it = itc[0]
                itc[0] += 1
                sb = (it % 2) * 2
                ob = 4 + (it % 3)
                Sps = PS[:, sb:sb + 2, :].rearrange("p b n -> p (b n)")
                n = len(chunks)

                def qk(e, hd=hd, hc=hc, hp=hp, Sps=Sps):
                    for ci, (kf, vf, bf) in enumerate(chunks):
                        r = e.matmul(Sps[:, ci * 128:(ci + 1) * 128], lhsT=kf(hp, hc), rhs=qsel(hp, hc),
                                     start=True, stop=(bf is None))
                        if bf is not None:
                            r = e.matmul(Sps[:, ci * 128:(ci + 1) * 128], lhsT=ident, rhs=bf(hd), start=False, stop=True)
                    return r
                P.op("pe", qk, reads=rkeys + ["ident"], writes=[psk(sb), psk(sb + 1)])
                pt = PT[it % 2]
                P.op("act", lambda e, pt=pt, Sps=Sps, n=n: e.activation(out=pt[:, 0:n, :].rearrange("p c q -> p (c q)"),
                                                                        in_=Sps[:, 0:n * 128], func=AF.Exp),
                     reads=[psk(sb), psk(sb + 1)], writes=[("PT", it % 2)])

                def pv(e, hd=hd, hp=hp, pt=pt, ob=ob, chunks=chunks, n=n):
                    for ci, (kf, vf, bf) in enumerate(chunks):
                        e.matmul(PS[hp:hp + 64, ob, 0:128], lhsT=vf(hd), rhs=pt[:, ci, :], start=(ci == 0), stop=(ci == n - 1))
                    for ci in range(n):
                        r = e.matmul(PS[hp:hp + 64, ob, 128:256], lhsT=ones64, rhs=pt[:, ci, :], start=(ci == 0),
                                     stop=(ci == n - 1))
                    return r

                def partB(pv=pv, hp=hp, hc=hc, ob=ob, it=it, rkeys=rkeys, okey=okey, osel=osel):
                    P.op("pe", pv, reads=rkeys + [("PT", it % 2), "ones64"], writes=[psk(ob)])
                    P.op("dve", lambda e: e.reciprocal(out=rinv[hp:hp + 64, :], in_=PS[hp:hp + 64, ob, 128:256]),
                         reads=[psk(ob)], writes=["rinv"])
                    P.op("dve", lambda e: e.tensor_tensor(out=osel(hp, hc), in0=PS[hp:hp + 64, ob, 0:128],
                                                          in1=rinv[hp:hp + 64, :], op=ALU.mult),
                         reads=[psk(ob), "rinv"], writes=[okey])
                if ATT_PIPE:
                    flush_att()
                pend.append(partB)
                if not ATT_PIPE:
                    flush_att()

        def wo_resid(nt, xkey):
            flush_att()
            for t in range(nt):
                def ymm(e, half, outap, t=t):
                    for k in range(KC):
                        r = e.matmul(outap, lhsT=oT[:, k, t * 128:(t + 1) * 128], rhs=Wo[:, k, half * 512:(half + 1) * 512],
                                     start=(k == 0), stop=(k == KC - 1))
                    return r
                post_resid(ymm, xt[:, t, :], xkey, wok + ["oT"])

        ctx_chunks = [((lambda hp, hc, cc=cc: kcT[hp:hp + 64, hc, cc * 128:(cc + 1) * 128]),
                       (lambda hd, cc=cc: vc[:, cc, hd * 64:(hd + 1) * 64]), None) for cc in range(2)]
        for qt in range(2):
            attend(ctx_chunks, lambda hp, hc, qt=qt: qcT[hp:hp + 64, hc, qt * 128:(qt + 1) * 128],
                   lambda hp, hc, qt=qt: oT[hp:hp + 64, hc, qt * 128:(qt + 1) * 128], ["qcT", "kcT", "vc"], "oT")
        wo_resid(2, "xt")
        xstore(xt[:, 0:2, :], 0, 2, xcs, "xt")
        load_modtiles(L, 1, 0, 1.0)
        for qbk in range(NG):
            ws = min(max(8 * qbk - 4, 0), NR - 16)
            t0 = ws * 64
            P.dma("sp", Kw, kTd[:, :, t0:t0 + 1024].rearrange("k p n -> p k n"),
                  reads=[("kTd", gg_) for gg_ in range(NG)], writes=["Kw"])
            P.dma("sp", Vw, vvd[t0:t0 + 1024, :].rearrange("(c p) d -> p c d", p=128),
                  reads=[("vvd", gg_) for gg_ in range(NG)], writes=["Vw"])
            P.dma("sp", qb, qTd[:, :, qbk * 512:(qbk + 1) * 512].rearrange("k p n -> p k n"), reads=[("qTd", qbk)],
                  writes=["qb"])
            xload(xt, qbk, 4, xsrc, "xt")
            for i in range(4):
                Pq = 4 * qbk + i
                bs = bias_set_of(Pq, NP)
                if bs == 0:
                    btile, bkey = bI, "bI"
                else:
                    btile, bkey = bSp[bs % 2], ("bSp", bs % 2)
                    P.dma("pool", btile.rearrange("p h o q -> p (h o q)"), na_bias[j, bs], writes=[bkey])
                Rs = pair_chunks(Pq, NR)
                chunks = []
                for oi, R in enumerate(Rs):
                    cw = R - ws // 2
                    assert 0 <= cw < 8
                    chunks.append(((lambda hp, hc, cw=cw: Kw[hp:hp + 64, hc, cw * 128:(cw + 1) * 128]),
                                   (lambda hd, cw=cw: Vw[:, cw, hd * 64:(hd + 1) * 64]),
                                   (lambda hd, oi=oi, btile=btile: btile[:, hd, oi, :])))
                chunks = chunks + ctx_chunks
                attend(chunks, lambda hp, hc, i=i: qb[hp:hp + 64, hc, i * 128:(i + 1) * 128],
                       lambda hp, hc, i=i: oT[hp:hp + 64, hc, i * 128:(i + 1) * 128],
                       ["Kw", "Vw", "qb", "kcT", "vc", bkey], "oT")
            wo_resid(4, "xt")
            xstore(xt, qbk, 4, xdst, "xt")
        P.barrier()
        A.reset(m0)

    def emit_lru(L, xsrc, xdst, need_ctx):
        j = L // 2
        m = A.mark()
        Wa = A.bf16(2 * LC * 128).rearrange("p (d c n) -> p d c n", d=2, c=LC)
        Wx = A.bf16(2 * LC * 128).rearrange("p (d c n) -> p d c n", d=2, c=LC)
        Wo = A.bf16(LC * D).rearrange("p (c n) -> p c n", c=LC)
        cw = A.f32(LC * 4).rearrange("p (t c) -> p t c", t=4)
        cb = A.f32(LC)
        ba = A.f32(2 * LC).rearrange("p (d c) -> p d c", d=2)
        bx = A.f32(2 * LC).rearrange("p (d c) -> p d c", d=2)
        nsp = A.f32(2 * LC).rearrange("p (d c) -> p d c", d=2)
        state = A.f32(2 * LC).rearrange("p (d c) -> p d c", d=2)
        zero_state = A.f32(LC)
        for d in range(2):
            P.dma("pool", Wa[:, d, :, :], l_wa[j, d].rearrange("c k n -> k c n"), writes=["wa"])
            P.dma("pool", Wx[:, d, :, :], l_wx[j, d].rearrange("c k n -> k c n"), writes=["wx"])
        for c in range(LC):
            P.dma("pool", Wo[:, c, :], l_wo[j, c * 128:(c + 1) * 128, :], writes=[("lwo", c)])
        for tap in range(4):
            P.dma("sp", cw[:, tap, :], l_cw[j, tap, :].rearrange("(c p) -> p c", p=128), writes=[("cw", tap)],
                  allow_slow_non_contiguous=True)
        P.dma("sp", cb, l_cb[j].rearrange("(c p) -> p c", p=128), writes=["cb"], allow_slow_non_contiguous=True)
        for d in range(2):
            P.dma("sp", ba[:, d, :], l_ba[j, d, :].rearrange("(c p) -> p c", p=128), writes=[("ba", d)],
                  allow_slow_non_contiguous=True)
            P.dma("sp", bx[:, d, :], l_bx[j, d, :].rearrange("(c p) -> p c", p=128), writes=[("bx", d)],
                  allow_slow_non_contiguous=True)
            P.dma("sp", nsp[:, d, :], l_lam[j, d, :].rearrange("(c p) -> p c", p=128), writes=[("nspd", d)],
                  allow_slow_non_contiguous=True)
        P.op("act", lambda e: e.activation(out=nsp, in_=nsp, func=AF.Exp, scale=-1.0), reads=[("nspd", 0), ("nspd", 1)], writes=["nsp"])
        P.op("act", lambda e: e.activation(out=nsp, in_=nsp, func=AF.Ln, bias=1.0), reads=["nsp"], writes=["nsp"])
        P.op("dve", lambda e: e.tensor_scalar(out=nsp, in0=nsp, scalar1=-8.0, scalar2=None, op0=ALU.mult),
             reads=["nsp"], writes=["nsp"])

        def phase(kind):
            nonlocal junk
            NSET = 1 if kind == "A" else 4
            if kind == "A":
                Win = A.bf16(KC * 2 * LW).rearrange("p (k n) -> p k n", k=KC)
                for k in range(KC):
                    P.dma("pool", Win[:, k, :], l_win[j, k * 128:(k + 1) * 128, :], writes=[("win", k)])
            hb = A.bf16(D) if kind == "A" else None
            junk = A.bf16(1024)
            hT = A.bf16(KC * 512).rearrange("p (k n) -> p k n", k=KC) if kind == "A" else None
            xt = A.f32(4 * D).rearrange("p (t d) -> p t d", t=4)
            ust = A.f32(LC * 512).rearrange("p (c n) -> p c n", c=LC)
            gst = A.bf16(LC * 512).rearrange("p (c n) -> p c n", c=LC)
            upw = A.f32(LC * 516).rearrange("p (c n) -> p c n", c=LC)
            ub = A.bf16(LC * 512).rearrange("p (c n) -> p c n", c=LC)
            hsc = A.f32(LC * 512).rearrange("p (c n) -> p c n", c=LC)
            hfl = upw[:, :, 0:512]
            rg = ub
            sets = [(A.f32(512), A.f32(512), A.f32(512), A.f32(512)) for _ in range(NSET)]
            wink = [("win", k) for k in range(KC)]
            lwok = [("lwo", c) for c in range(LC)]

            def proj_in(n):
                for c in range(LC):
                    b1, b2 = c % 2, 2 + c % 2

                    def mm(e, c=c, b=b1, cbase=LW):
                        for k in range(KC):
                            r = e.matmul(PS[:, b, 0:n], lhsT=Win[:, k, cbase + c * 128:cbase + (c + 1) * 128], rhs=hT[:, k, 0:n],
                                         start=(k == 0), stop=(k == KC - 1))
                        return r

                    def mm2(e, c=c, b=b2, cbase=0):
                        for k in range(KC):
                            r = e.matmul(PS[:, b, 0:n], lhsT=Win[:, k, cbase + c * 128:cbase + (c + 1) * 128], rhs=hT[:, k, 0:n],
                                         start=(k == 0), stop=(k == KC - 1))
                        return r
                    P.op("pe", mm, reads=wink + ["hT"], writes=[psk(b1)])
                    P.op("pe", mm2, reads=wink + ["hT"], writes=[psk(b2)])
                    P.op("dve", lambda e, c=c, b=b1: e.tensor_copy(out=ust[:, c, 0:n], in_=PS[:, b, 0:n]),
                         reads=[psk(b1)], writes=["ust"])
                    P.op("act", lambda e, c=c, b=b2: e.activation(out=gst[:, c, 0:n], in_=PS[:, b, 0:n], func=AF.Gelu),
                         reads=[psk(b2)], writes=["gst"])

            def conv(n):
                for c in range(LC):
                    P.op("dve", lambda e, c=c: e.tensor_scalar(out=ust[:, c, 0:n], in0=upw[:, c, 0:n], scalar1=cw[:, 0, c:c + 1],
                                                                scalar2=cb[:, c:c + 1], op0=ALU.mult, op1=ALU.add),
                         reads=["upw", "cb"] + [("cw", tp) for tp in range(4)], writes=["ust"])
                    for tap in range(1, 4):
                        P.op("dve", lambda e, c=c, tap=tap: e.scalar_tensor_tensor(
                            out=ust[:, c, 0:n], in0=upw[:, c, tap:tap + n], scalar=cw[:, tap, c:c + 1], in1=ust[:, c, 0:n],
                            op0=ALU.mult, op1=ALU.add), reads=["upw", ("cw", tap), "ust"], writes=["ust"])
                P.op("pool", lambda e: e.tensor_copy(out=ub[:, :, 0:n], in_=ust[:, :, 0:n]), reads=["ust"], writes=["ub"])

            def gates_scan(d, n, init_ap, init_key):
                sk = [("state", d, c_) for c_ in range(LC)]
                for c in range(LC):
                    b1, b2 = c % 2, 2 + c % 2
                    si = c % NSET
                    at, it_, sq, bt = sets[si]
                    kat, kit, ksq, kbt = ("at", si), ("it", si), ("sq", si), ("bt", si)
                    P.op("pe", lambda e, c=c, b=b1: e.matmul(PS[:, b, 0:n], lhsT=Wa[:, d, c, :], rhs=ub[:, c, 0:n],
                                                             start=True, stop=True), reads=["wa", "ub"], writes=[psk(b1)])
                    P.op("pe", lambda e, c=c, b=b2: e.matmul(PS[:, b, 0:n], lhsT=Wx[:, d, c, :], rhs=ub[:, c, 0:n],
                                                             start=True, stop=True), reads=["wx", "ub"], writes=[psk(b2)])
                    P.op("act", lambda e, c=c, b=b1, at=at: e.activation(out=at[:, 0:n], in_=PS[:, b, 0:n], func=AF.Sigmoid,
                                                                  bias=ba[:, d, c:c + 1]), reads=[psk(b1), ("ba", d)], writes=[kat])
                    P.op("act", lambda e, c=c, at=at: e.activation(out=at[:, 0:n], in_=at[:, 0:n], func=AF.Exp,
                                                            scale=nsp[:, d, c:c + 1]), reads=[kat, "nsp"], writes=[kat])
                    P.op("act", lambda e, c=c, b=b2, it_=it_: e.activation(out=it_[:, 0:n], in_=PS[:, b, 0:n], func=AF.Sigmoid,
                                                                  bias=bx[:, d, c:c + 1]), reads=[psk(b2), ("bx", d)], writes=[kit])
                    P.op("act", lambda e, sq=sq, at=at: e.activation(out=sq[:, 0:n], in_=at[:, 0:n], func=AF.Square), reads=[kat], writes=[ksq])
                    P.op("act", lambda e, sq=sq: e.activation(out=sq[:, 0:n], in_=sq[:, 0:n], func=AF.Sqrt, scale=-1.0, bias=1.0),
                         reads=[ksq], writes=[ksq])
                    P.op("pool", lambda e, c=c, it_=it_: e.tensor_tensor(out=it_[:, 0:n], in0=it_[:, 0:n], in1=ust[:, c, 0:n], op=ALU.mult),
                         reads=[kit, "ust"], writes=[kit])
                    P.op("pool", lambda e, bt=bt, it_=it_, sq=sq: e.tensor_tensor(out=bt[:, 0:n], in0=it_[:, 0:n], in1=sq[:, 0:n], op=ALU.mult),
                         reads=[kit, ksq], writes=[kbt])
                    ini = init_ap(c)
                    if d == 0:
                        P.op("dve", lambda e, c=c, ini=ini, at=at, bt=bt: e.tensor_tensor_scan(out=hsc[:, c, 0:n], data0=at[:, 0:n], data1=bt[:, 0:n],
                                                                                initial=ini, op0=ALU.mult, op1=ALU.add),
                             reads=[kat, kbt, init_key(c)], writes=[("hsc", c)])
                        P.op("pool", lambda e, c=c: e.tensor_copy(out=state[:, d, c:c + 1], in_=hsc[:, c, n - 1:n]),
                             reads=[("hsc", c)], writes=[("state", d, c)])
                    else:
                        P.op("dve", lambda e, c=c, ini=ini, at=at, bt=bt: e.tensor_tensor_scan(out=hsc[:, c, 0:n][:, ::-1],
                                                                                data0=at[:, 0:n][:, ::-1], data1=bt[:, 0:n][:, ::-1],
                                                                                initial=ini, op0=ALU.mult, op1=ALU.add),
                             reads=[kat, kbt, init_key(c)], writes=[("hsc", c)])
                        P.op("pool", lambda e, c=c: e.tensor_copy(out=state[:, d, c:c + 1], in_=hsc[:, c, 0:1]),
                             reads=[("hsc", c)], writes=[("state", d, c)])

            def out_resid(nt, xkey):
                for t in range(nt):
                    def ymm(e, half, outap, t=t):
                        for c in range(LC):
                            r = e.matmul(outap, lhsT=rg[:, c, t * 128:(t + 1) * 128], rhs=Wo[:, c, half * 512:(half + 1) * 512],
                                         start=(c == 0), stop=(c == LC - 1))
                        return r
                    post_resid(ymm, xt[:, t, :], xkey, lwok + ["ub"])

            if kind == "A":
                load_modtiles(L, 1, 1, 1.0)
                xload(xt[:, 0:2, :], 0, 2, xcs, "xt")
                prep_h(xt, "xt", 2, hb, hT, "hT")
                proj_in(256)
                P.op("pool", lambda e: e.memset(upw, 0.0), writes=["upw"])
                P.op("pool", lambda e: e.tensor_copy(out=upw[:, :, 2:258], in_=ust[:, :, 0:256]), reads=["ust", "upw"], writes=["upw"])
                conv(256)
                P.op("pool", lambda e: e.memset(state, 0.0), writes=[("state", d_, c_) for d_ in range(2) for c_ in range(LC)])
                P.op("pool", lambda e: e.memset(zero_state, 0.0), writes=["zs"])
                gates_scan(0, 256, lambda c: zero_state[:, c:c + 1], lambda c: "zs")
                P.op("pool", lambda e: e.tensor_copy(out=hfl[:, :, 0:256], in_=hsc[:, :, 0:256]), reads=[("hsc", c_) for c_ in range(LC)], writes=["upw"])
                gates_scan(1, 256, lambda c: zero_state[:, c:c + 1], lambda c: "zs")
                if need_ctx:
                    P.op("dve", lambda e: e.tensor_tensor(out=hfl[:, :, 0:256], in0=hfl[:, :, 0:256], in1=hsc[:, :, 0:256], op=ALU.add),
                         reads=["upw"] + [("hsc", c_) for c_ in range(LC)], writes=["upw"])
                    P.op("dve", lambda e: e.tensor_tensor(out=rg[:, :, 0:256], in0=hfl[:, :, 0:256], in1=gst[:, :, 0:256], op=ALU.mult),
                         reads=["upw", "gst"], writes=["ub"])
                    out_resid(2, "xt")
                    xstore(xt[:, 0:2, :], 0, 2, xcs, "xt")
                load_modtiles(L, 1, 0, 1.0)
                for g in range(NG):
                    xload(xt, g, 4, xsrc, "xt")
                    prep_h(xt, "xt", 4, hb, hT, "hT")
                    proj_in(512)
                    P.dma("sp", upre[:, :, g * 512:(g + 1) * 512].rearrange("c p n -> p c n"), ust, reads=["ust"], writes=[("upre", g)])
                    P.dma("sp", ggd[:, :, g * 512:(g + 1) * 512].rearrange("c p n -> p c n"), gst, reads=["gst"], writes=[("ggd", g)])
            else:
                for g in range(NG):
                    t0 = g * 512
                    lo, hi = max(t0 - 2, 0), min(t0 + 514, S)
                    if g == 0 or g == NG - 1:
                        P.op("pool", lambda e: e.memset(upw, 0.0), writes=["upw"])
                    P.dma("sp", upw[:, :, lo - (t0 - 2):hi - (t0 - 2)], upre[:, :, lo:hi].rearrange("c p n -> p c n"),
                          reads=[("upre", gg_) for gg_ in range(max(g - 1, 0), min(g + 2, NG))], writes=["upw"])
                    conv(512)
                    P.dma("sp", ubuf[:, :, t0:t0 + 512].rearrange("c p n -> p c n"), ust, reads=["ust"], writes=[("ubuf", g)])
                    gates_scan(0, 512, lambda c: state[:, 0, c:c + 1], lambda c: ("state", 0, c))
                    P.dma("sp", hfd[:, :, t0:t0 + 512].rearrange("c p n -> p c n"), hsc, reads=[("hsc", c_) for c_ in range(LC)], writes=[("hfd", g)])
                for g in range(NG - 1, -1, -1):
                    t0 = g * 512
                    P.dma("sp", ust, ubuf[:, :, t0:t0 + 512].rearrange("c p n -> p c n"), reads=[("ubuf", g)], writes=["ust"])
                    P.op("pool", lambda e: e.tensor_copy(out=ub, in_=ust), reads=["ust"], writes=["ub"])
                    P.dma("sp", hfl, hfd[:, :, t0:t0 + 512].rearrange("c p n -> p c n"), reads=[("hfd", g)], writes=["upw"])
                    P.dma("sp", gst, ggd[:, :, t0:t0 + 512].rearrange("c p n -> p c n"), reads=[("ggd", g)], writes=["gst"])
                    xload(xt, g, 4, xsrc, "xt")
                    gates_scan(1, 512, lambda c: state[:, 1, c:c + 1], lambda c: ("state", 1, c))
                    P.op("dve", lambda e: e.tensor_tensor(out=hfl, in0=hfl, in1=hsc, op=ALU.add), reads=["upw"] + [("hsc", c_) for c_ in range(LC)], writes=["upw"])
                    P.op("dve", lambda e: e.tensor_tensor(out=rg, in0=hfl, in1=gst, op=ALU.mult), reads=["upw", "gst"], writes=["ub"])
                    out_resid(4, "xt")
                    xstore(xt, g, 4, xdst, "xt")


        mA = A.mark()
        phase("A")
        P.barrier()
        A.reset(mA)
        phase("B")
        P.barrier()
        A.reset(m)

    nl = len(layers)
    import os
    cut = int(os.environ.get("MK_CUT", "99"))
    for li, L in enumerate(layers):
        last = li == nl - 1
        emit_mod(L)
        if cut == 0:
            break
        emit_ffn(L, 0, x_in if li == 0 else xs, xs, ctx_in if li == 0 else xcs, True, False)
        if cut == 1:
            break
        if L % 2 == 0:
            emit_na(L, xs, xs)
        else:
            emit_lru(L, xs, xs, not last)
        if cut == 2:
            break
        emit_ffn(L, 2, xs, out if last else xs, xcs, not last, last)
    P.emit(final_wait_ops=final_stores)
    nc._prog = P
    return nc


def rope_tables(S):
    t = np.arange(S)
    row = (t // 64).astype(np.float32)
    col = (t % 64).astype(np.float32)
    inv = np.power(np.float32(10000.0), -np.arange(16, dtype=np.float32) / np.float32(16)).astype(np.float32)
    ang = np.concatenate([row[:, None] * inv, col[:, None] * inv], axis=-1).astype(np.float32)
    cos, sin = np.cos(ang).astype(np.float32), np.sin(ang).astype(np.float32)
    C64 = np.concatenate([cos, cos], axis=1).T
    S64 = np.concatenate([-sin, sin], axis=1).T
    return (np.ascontiguousarray(np.concatenate([C64, C64], 0)), np.ascontiguousarray(np.concatenate([S64, S64], 0)))


def build_bias(rpb, NR):
    NP = NR // 2
    reps = [2, 0, 1, NP - 2, NP - 1]
    outb = np.full((5, 128, 16, 5, 128), NEG, np.float32)
    ql = np.arange(128)
    qr, qc = ql // 64, ql % 64
    kl = np.arange(128)
    kp, kc = kl // 64, kl % 64
    for si, Pq in enumerate(reps):
        Rs = pair_chunks(Pq, NR)
        r = 2 * Pq + qr
        r0 = np.clip(r - 4, 0, NR - 8)
        c0 = np.clip(qc - 8, 0, 48)
        for oi, R in enumerate(Rs):
            kr = 2 * R + kp
            valid = ((kr[:, None] >= r0[None, :]) & (kr[:, None] < r0[None, :] + 8) &
                     (kc[:, None] >= c0[None, :]) & (kc[:, None] < c0[None, :] + 16))
            dr = np.clip(kr[:, None] - r[None, :] + 7, 0, 14)
            dc = np.clip(kc[:, None] - qc[None, :] + 15, 0, 30)
            vals = rpb[:, dr, dc]
            outb[si, :, :, oi, :] = np.where(valid[None], vals, np.float32(NEG)).transpose(1, 0, 2)
    return outb.reshape(5, 128, 16 * 5 * 128)


def host_inputs(inp, b, S):
    f = lambda a: np.ascontiguousarray(np.asarray(a, dtype=np.float32))
    NR = S // 64
    wq = f(inp["na_w_qkv"])
    perm = (np.arange(D) // 64) * 64 + (np.arange(D) % 64 + 32) % 64
    q, k, v = wq[:, :, 0:D], wq[:, :, D:2 * D], wq[:, :, 2 * D:3 * D]
    na_w_ext = np.concatenate([q, q[:, :, perm], k, k[:, :, perm], v], axis=2)
    rC, rS = rope_tables(S)
    rpb = f(inp["na_rpb"])
    d = dict(
        x=f(inp["x"][b]), ctx=f(inp["ctx"][b]), cvec=f(np.stack([inp["c"][b], inp["c_ctx"]], 0)),
        ada_w=f(inp["ada_w"]), ada_b=f(inp["ada_b"]), norm_pre=f(inp["norm_pre"]), norm_post=f(inp["norm_post"]),
        ffn_w_gate=f(inp["ffn_w_gate"]), ffn_w_up=f(inp["ffn_w_up"]), ffn_w_down=f(inp["ffn_w_down"]),
        na_w_ext=f(na_w_ext), na_w_o=f(inp["na_w_o"]),
        na_bias=f(np.stack([build_bias(rpb[j], NR) for j in range(2)], 0)),
        lru_w_in=f(inp["lru_w_in"]), lru_conv_w=f(inp["lru_conv_w"]), lru_conv_b=f(inp["lru_conv_b"]),
        lru_w_a=f(inp["lru_w_a"]), lru_b_a=f(inp["lru_b_a"]), lru_w_x=f(inp["lru_w_x"]), lru_b_x=f(inp["lru_b_x"]),
        lru_lambda=f(inp["lru_lambda"]), lru_w_o=f(inp["lru_w_o"]),
        ropeC=rC, ropeS=rS, ident=np.eye(128, dtype=np.float32),
    )
    return d


def kernel(**inputs):
    x = np.asarray(inputs["x"])
    B, S, _ = x.shape
    nc = build(S)
    in_maps = [host_inputs(inputs, b, S) for b in range(B)]
    res = run_bass_kernel_spmd(nc, in_maps, core_ids=list(range(B)))
    return np.stack([np.asarray(res.results[b]["out"]) for b in range(B)], 0).astype(np.float32)
```
